# Optimizing a Trainium2 kernel written in Bass

```python
import jax, jax.numpy as jnp
from jax import lax
import numpy as np

D_MODEL = 2048
BATCH = 8
SEQ = 4096
DEPTH = 4

GRID_W = 64
CTX_LEN = 256
EXPAND = 2
D_INNER = EXPAND * D_MODEL
A_WIDTH = D_INNER // 2
H_A = 8
DV_A = A_WIDTH // H_A
DK_A = DV_A // 2
QK_WIDTH = H_A * DK_A
B_WIDTH = D_INNER - A_WIDTH
G_B = 8
GB_DIM = B_WIDTH // G_B
EVEN_IN = 2 * QK_WIDTH + 2 * A_WIDTH + 2 * B_WIDTH
C_WIDTH = D_INNER
ODD_IN = 4 * C_WIDTH
CONV_W = 3
CHUNK = 128
ROPE_BASE = 10000.0
EPS = 1e-6
N_EVEN = (DEPTH + 1) // 2
N_ODD = DEPTH // 2

kernel_name = "hybrid_retnet_fnet_shortconv_prefix_dit"


def _rmsnorm(x, w):
    xf = x.astype(jnp.float32)
    y = xf * lax.rsqrt(jnp.mean(xf * xf, axis=-1, keepdims=True) + EPS)
    return (y * w.astype(jnp.float32)).astype(x.dtype)


def _modulate(x, w, shift, scale):
    return _rmsnorm(x, w) * (1.0 + scale) + shift


def _heads(t, d):
    b, L, _ = t.shape
    return t.reshape(b, L, -1, d).transpose(0, 2, 1, 3).astype(jnp.float32)


def _axial_rope(t):
    L = t.shape[2]
    pos = jnp.arange(L)
    row = (pos // GRID_W).astype(jnp.float32)
    col = (pos % GRID_W).astype(jnp.float32)
    nf = DK_A // 4
    inv = ROPE_BASE ** (-jnp.arange(nf, dtype=jnp.float32) / nf)
    ang = jnp.concatenate([row[:, None] * inv[None], col[:, None] * inv[None]], axis=-1)
    cos, sin = jnp.cos(ang), jnp.sin(ang)
    t1, t2 = t[..., : DK_A // 2], t[..., DK_A // 2:]
    return jnp.concatenate([t1 * cos - t2 * sin, t1 * sin + t2 * cos], axis=-1)


def _retention_scan(q, k, v, log_gamma, s0, inclusive):
    b, h, L, dk = q.shape
    dv = v.shape[-1]
    nc = L // CHUNK
    qc = q.reshape(b, h, nc, CHUNK, dk)
    kc = k.reshape(b, h, nc, CHUNK, dk)
    vc = v.reshape(b, h, nc, CHUNK, dv)
    idx = jnp.arange(CHUNK, dtype=jnp.float32)
    rel = idx[:, None] - idx[None, :]
    keep = (rel >= 0) if inclusive else (rel > 0)
    dmat = jnp.where(keep[None], jnp.exp(log_gamma[:, None, None] * jnp.maximum(rel, 0.0)[None]), 0.0)
    scores = jnp.einsum('bhnid,bhnjd->bhnij', qc, kc) * dmat[None, :, None]
    o = jnp.einsum('bhnij,bhnje->bhnie', scores, vc)
    zeta = jnp.exp(log_gamma[:, None] * (CHUNK - 1.0 - idx)[None])
    inc = jnp.einsum('bhnjd,bhnje->bhnde', kc, vc * zeta[None, :, None, :, None])
    chunk_decay = jnp.exp(log_gamma * CHUNK)[None, :, None, None]

    def step(s, inc_n):
        return chunk_decay * s + inc_n, s

    s_last, s_prev = lax.scan(step, s0, jnp.moveaxis(inc, 2, 0))
    s_prev = jnp.moveaxis(s_prev, 0, 2)
    xi = jnp.exp(log_gamma[:, None] * (idx + 1.0)[None])
    o = o + jnp.einsum('bhnid,bhnde->bhnie', qc, s_prev) * xi[None, :, None, :, None]
    return o.reshape(b, h, L, dv), s_last


def _retention(q, k, v, log_gamma, s0_fwd, s0_bwd):
    flip = lambda t: jnp.flip(t, axis=2)
    o_f, s_f = _retention_scan(q, k, v, log_gamma[0], s0_fwd, True)
    o_b, s_b = _retention_scan(flip(q), flip(k), flip(v), log_gamma[1], s0_bwd, False)
    o = o_f + flip(o_b)
    o = o * lax.rsqrt(jnp.mean(o * o, axis=-1, keepdims=True) + EPS)
    return o, s_f, s_b


def _merge_heads(o, dtype):
    b, h, L, d = o.shape
    return o.transpose(0, 2, 1, 3).reshape(b, L, h * d).astype(dtype)


def _fourier(u, fno_w):
    b, L, _ = u.shape
    ug = u.reshape(b, L, G_B, GB_DIM).astype(jnp.float32)
    f = jnp.fft.fft2(ug, axes=(1, 3), norm='ortho').real
    y = jnp.einsum('blgc,gcd->blgd', f.astype(u.dtype), fno_w)
    return y.reshape(b, L, B_WIDTH)


def _split_even(p):
    cuts = [QK_WIDTH, 2 * QK_WIDTH, 2 * QK_WIDTH + A_WIDTH, 2 * QK_WIDTH + 2 * A_WIDTH,
            2 * QK_WIDTH + 2 * A_WIDTH + B_WIDTH]
    return jnp.split(p, cuts, axis=-1)


def _even_layer(hx, hc, w_in, decay_logit, fno_w, w_out, need_ctx):
    log_gamma = -jnp.exp(decay_logit.astype(jnp.float32))
    qx, kx, vx, gax, ux, gbx = _split_even(hx @ w_in)
    qc, kc, vc, gac, uc, gbc = _split_even(hc @ w_in)
    kscale = DK_A ** -0.5
    b = hx.shape[0]
    zeros = jnp.zeros((b, H_A, DK_A, DV_A), jnp.float32)
    ret_c, s_f, s_b = _retention(_heads(qc, DK_A), _heads(kc, DK_A) * kscale, _heads(vc, DV_A),
                                 log_gamma, zeros, zeros)
    ret_x, _, _ = _retention(_axial_rope(_heads(qx, DK_A)), _axial_rope(_heads(kx, DK_A)) * kscale,
                             _heads(vx, DV_A), log_gamma, s_f, s_b)
    yx = jnp.concatenate([_merge_heads(ret_x, hx.dtype) * jax.nn.silu(gax),
                          _fourier(ux, fno_w) * jax.nn.silu(gbx)], axis=-1) @ w_out
    yc = None
    if need_ctx:
        yc = jnp.concatenate([_merge_heads(ret_c, hc.dtype) * jax.nn.silu(gac),
                              _fourier(uc, fno_w) * jax.nn.silu(gbc)], axis=-1) @ w_out
    return yx, yc


def _conv3(z, w, rows):
    b, L, ch = z.shape
    if rows is None:
        zp = jnp.pad(z, ((0, 0), (1, 1), (0, 0)))
        return zp[:, :-2] * w[0] + zp[:, 1:-1] * w[1] + zp[:, 2:] * w[2]
    zg = z.reshape(b, rows, GRID_W, ch)
    zp = jnp.pad(zg, ((0, 0), (0, 0), (1, 1), (0, 0)))
    y = zp[:, :, :-2] * w[0] + zp[:, :, 1:-1] * w[1] + zp[:, :, 2:] * w[2]
    return y.reshape(b, L, ch)


def _odd_layer(h, w_in, conv_w, w_out, rows):
    bg, cg, xt, g = jnp.split(h @ w_in, 4, axis=-1)
    y = bg * _conv3(cg * xt, conv_w, rows) * jax.nn.silu(g)
    return y @ w_out


def setup_inputs(seed: int = 0) -> dict:
    key = jax.random.key(seed)
    ks = jax.random.split(key, 15)

    def nrm(k, shape, scale):
        return jax.random.normal(k, shape, jnp.float32) * scale

    base = jnp.asarray(np.log(-np.log1p(-2.0 ** (-5.0 - np.arange(H_A)))), jnp.float32)
    return {
        'x': nrm(ks[0], (BATCH, SEQ, D_MODEL), 1.0),
        'c': nrm(ks[1], (BATCH, D_MODEL), 1.0),
        'ctx': nrm(ks[2], (BATCH, CTX_LEN, D_MODEL), 1.0),
        'c_ctx': nrm(ks[3], (D_MODEL,), 1.0),
        'ada_w': nrm(ks[4], (DEPTH, D_MODEL, 3 * D_MODEL), 0.5 * D_MODEL ** -0.5),
        'ada_b': nrm(ks[5], (DEPTH, 3 * D_MODEL), 0.02),
        'norm_w': 1.0 + nrm(ks[6], (DEPTH, D_MODEL), 0.02),
        'ev_w_in': nrm(ks[7], (N_EVEN, D_MODEL, EVEN_IN), D_MODEL ** -0.5),
        'ret_decay_logit': base[None, None, :] + nrm(ks[8], (N_EVEN, 2, H_A), 0.1),
        'fno_w': nrm(ks[9], (N_EVEN, G_B, GB_DIM, GB_DIM), GB_DIM ** -0.5),
        'ev_w_out': nrm(ks[10], (N_EVEN, D_INNER, D_MODEL), D_INNER ** -0.5),
        'od_w_in': nrm(ks[11], (N_ODD, D_MODEL, ODD_IN), D_MODEL ** -0.5),
        'conv_w': nrm(ks[12], (N_ODD, CONV_W, C_WIDTH), CONV_W ** -0.5),
        'od_w_out': nrm(ks[13], (N_ODD, C_WIDTH, D_MODEL), C_WIDTH ** -0.5),
        'final_norm_w': 1.0 + nrm(ks[14], (D_MODEL,), 0.02),
    }


def reference(x, c, ctx, c_ctx, ada_w, ada_b, norm_w, ev_w_in, ret_decay_logit, fno_w, ev_w_out,
              od_w_in, conv_w, od_w_out, final_norm_w):
    rows = x.shape[1] // GRID_W
    sc = jax.nn.silu(c)
    scc = jax.nn.silu(c_ctx)
    for i in range(DEPTH):
        need_ctx = i < DEPTH - 1
        shift_x, scale_x, gate_x = jnp.split((sc @ ada_w[i] + ada_b[i])[:, None, :], 3, axis=-1)
        shift_c, scale_c, gate_c = jnp.split(scc @ ada_w[i] + ada_b[i], 3, axis=-1)
        hx = _modulate(x, norm_w[i], shift_x, scale_x)
        hc = _modulate(ctx, norm_w[i], shift_c, scale_c)
        j = i // 2
        if i % 2 == 0:
            yx, yc = _even_layer(hx, hc, ev_w_in[j], ret_decay_logit[j], fno_w[j], ev_w_out[j], need_ctx)
        else:
            yx = _odd_layer(hx, od_w_in[j], conv_w[j], od_w_out[j], rows)
            yc = _odd_layer(hc, od_w_in[j], conv_w[j], od_w_out[j], None) if need_ctx else None
        x = x + gate_x * yx
        if need_ctx:
            ctx = ctx + gate_c * yc
    return _rmsnorm(x, final_norm_w)
```

```python
import math
import numpy as np
import ml_dtypes
import concourse.bass as bass
import concourse.mybir as mybir
from concourse.bass_utils import run_bass_kernel_spmd

F32 = mybir.dt.float32
BF16 = mybir.dt.bfloat16
AF = mybir.ActivationFunctionType
ALU = mybir.AluOpType

D = 2048
SEQ = 4096
CTX = 256
NT = (SEQ + CTX) // 128
DEPTH = 4
EVEN_IN = 10240
ODD_IN = 16384
EPS = 1e-6
SB_BASE = 16512
SB_END = 229376
SEM_WRAP = 32000
STOP_AFTER = None


class Buf:
    __slots__ = ("name", "writers", "readers")

    def __init__(self, name):
        self.name = name
        self.writers = []
        self.readers = []


class TB:
    __slots__ = ("t", "b")

    def __init__(self, t, name):
        self.t = t
        self.b = Buf(name)

    def __getitem__(self, k):
        return self.t[k]


class Prog:
    ENG = ("pe", "act", "dve", "pool", "sp")

    def __init__(self, nc):
        self.nc = nc
        self.ops = []
        self.last = {e: None for e in self.ENG}
        self.phase_dmas = {}
        self.semkey_slot = {}
        self.n_dma_slots = 0
        self.max_dma_slots = 0
        self.sb_persist = SB_BASE
        self.sb_ptr = SB_BASE
        self.nalloc = 0
        self.bg_slot = {}
        self.n_fam = {}
        self.keepalive = []

    def _alloc(self, name, shape, dtype, persist):
        esz = 4 if dtype == F32 else 2
        n = esz
        for s in shape[1:]:
            n *= s
        n = (n + 63) // 64 * 64
        off = self.sb_ptr
        assert off + n <= SB_END, f"SBUF overflow allocating {name}: {off + n - SB_END} bytes over"
        self.sb_ptr = off + n
        if persist:
            assert self.sb_persist == off, "persistent allocs must precede phase allocs"
            self.sb_persist = self.sb_ptr
        self.nalloc += 1
        t = self.nc.alloc_sbuf_tensor_at(f"{name}_{self.nalloc}", list(shape), dtype, offset=off)
        return TB(t, name)

    def persist(self, name, shape, dtype):
        return self._alloc(name, shape, dtype, True)

    def tile(self, name, shape, dtype):
        return self._alloc(name, shape, dtype, False)

    def op(self, eng, fn, reads=(), writes=(), dma=False, semkey=None, disjoint=False, bg=False):
        i = len(self.ops)
        deps = set()
        rb = [x.b if isinstance(x, TB) else x for x in reads]
        wb = [x.b if isinstance(x, TB) else x for x in writes]
        for b in rb:
            deps.update(b.writers)
        for b in wb:
            deps.update(b.readers)
            if not disjoint:
                deps.update(b.writers)
        for b in wb:
            if b.readers or not disjoint:
                b.writers = [i]
                b.readers = []
            else:
                b.writers.append(i)
        for b in rb:
            if not any(b is w for w in wb):
                b.readers.append(i)
        deps.discard(i)
        best = {}
        keep = set()
        for d in deps:
            od = self.ops[d]
            if od["dma"]:
                keep.add(d)
            elif od["fn"] is not None:
                if best.get(od["eng"], -1) < d:
                    best[od["eng"]] = d
        keep.update(best.values())
        deps = keep
        o = dict(eng=eng, fn=fn, deps=deps, dma=dma, signal=False, slot=None, bg=bg)
        if dma and bg:
            k = id(semkey.b if isinstance(semkey, TB) else semkey)
            if k not in self.bg_slot:
                self.bg_slot[k] = 10000 + len(self.bg_slot)
            o["slot"] = self.bg_slot[k]
        elif dma:
            k = (id(semkey.b if isinstance(semkey, TB) else semkey), eng)
            if k not in self.semkey_slot:
                fam = 5000 if eng == "pool" else 0
                n = self.n_fam.get(fam, 0)
                self.semkey_slot[k] = fam + n
                self.n_fam[fam] = n + 1
            o["slot"] = self.semkey_slot[k]
            self.phase_dmas[o["slot"]] = i
        self.ops.append(o)
        self.last[eng] = i
        return i

    def bgdma(self, out, in_, reads, writes, semkey):
        return self.op("pool", lambda e: e.dma_start(out=out, in_=in_), reads, writes, dma=True, semkey=semkey,
                       disjoint=True, bg=True)

    def barrier(self, keep=None):
        tails = set(v for v in self.last.values() if v is not None and not self.ops[v].get("bg"))
        tails.update(self.phase_dmas.values())
        for e in self.ENG:
            i = len(self.ops)
            self.ops.append(dict(eng=e, fn=None, deps=set(t for t in tails), dma=False, signal=False, slot=None))
            self.last[e] = i
        self.phase_dmas = {}
        self.semkey_slot = {}
        self.n_fam = {}
        self.sb_ptr = self.sb_persist if keep is None else keep

    def mm(self, out, lhsT, rhs, start, stop, reads, writes):
        return self.op("pe", lambda e: e.matmul(out, lhsT, rhs, start=start, stop=stop), reads, writes, disjoint=True)

    def tr(self, out, in_, ident, reads, writes):
        return self.op("pe", lambda e: e.transpose(out, in_, ident), reads, writes, disjoint=True)

    def act(self, out, in_, func, reads, writes, bias=None, scale=None, accum=None, disjoint=False):
        kw = {}
        if bias is not None:
            kw["bias"] = bias
        if scale is not None:
            kw["scale"] = scale
        if accum is not None:
            kw["accum_out"] = accum
        return self.op("act", lambda e: e.activation(out, in_, func, **kw), reads, writes, disjoint=disjoint)

    def tt(self, eng, out, in0, in1, op, reads, writes, disjoint=False):
        return self.op(eng, lambda e: e.tensor_tensor(out, in0, in1, op), reads, writes, disjoint=disjoint)

    def ts(self, eng, out, in0, s1, s2, op0, op1, reads, writes, disjoint=False):
        if s2 is None:
            return self.op(eng, lambda e: e.tensor_scalar(out, in0, s1, None, op0), reads, writes, disjoint=disjoint)
        return self.op(eng, lambda e: e.tensor_scalar(out, in0, s1, s2, op0, op1), reads, writes, disjoint=disjoint)

    def stt(self, eng, out, in0, scalar, in1, op0, op1, reads, writes, disjoint=False):
        return self.op(eng, lambda e: e.scalar_tensor_tensor(out, in0, scalar, in1, op0, op1), reads, writes,
                       disjoint=disjoint)

    def copy(self, eng, out, in_, reads, writes, disjoint=False):
        if eng == "act":
            return self.op("act", lambda e: e.copy(out, in_), reads, writes, disjoint=disjoint)
        return self.op(eng, lambda e: e.tensor_copy(out, in_), reads, writes, disjoint=disjoint)

    def memset(self, eng, ap, val, writes):
        return self.op(eng, lambda e: e.memset(ap, val), (), writes)

    def recip(self, out, in_, reads, writes):
        return self.op("dve", lambda e: e.reciprocal(out, in_), reads, writes)

    def dma(self, eng, out, in_, reads, writes, semkey, disjoint=True):
        return self.op(eng, lambda e: e.dma_start(out=out, in_=in_), reads, writes, dma=True, semkey=semkey,
                       disjoint=disjoint)

    def load(self, dst_tb, dst_ap, src_ap, reads=()):
        return self.dma("sp", dst_ap, src_ap, reads, [dst_tb], dst_tb)

    def store(self, dst_ap, src_tb, src_ap, writes=()):
        return self.dma("pool", dst_ap, src_ap, [src_tb], writes, src_tb)

    def emit(self, stack):
        nc = self.nc
        ops = self.ops
        W = SEM_WRAP
        for o in ops:
            for d in o["deps"]:
                od = ops[d]
                if od["dma"] or od["fn"] is None:
                    continue
                if od["eng"] == "pe" and o["eng"] == "pe":
                    continue
                od["signal"] = True
        cnt = {e: 0 for e in self.ENG}
        dcnt = {}
        for o in ops:
            if o["dma"]:
                dcnt[o["slot"]] = dcnt.get(o["slot"], 0) + 16
                o["sv"] = dcnt[o["slot"]]
            elif o["signal"]:
                cnt[o["eng"]] += 1
                o["sv"] = cnt[o["eng"]]
        esems = {e: [stack.enter_context(nc.semaphore(f"s_{e}{k}")) for k in range((max(cnt[e], 1) - 1) // W + 1)]
                 for e in self.ENG}
        dsems = {s: [stack.enter_context(nc.semaphore(f"s_d{s}_{k}")) for k in range((max(c, 1) - 1) // W + 1)]
                 for s, c in dcnt.items()}
        self.n_sems = sum(len(v) for v in esems.values()) + sum(len(v) for v in dsems.values())
        self.counts = (dict(cnt), dict(dcnt))

        def local(o):
            v = o["sv"]
            k = (v - 1) // W
            if o["dma"]:
                return dsems[o["slot"]][k], v - k * W, ("d", o["slot"], k)
            return esems[o["eng"]][k], v - k * W, ("e", o["eng"], k)

        per_eng = {e: [] for e in self.ENG}
        for i, o in enumerate(ops):
            per_eng[o["eng"]].append(i)

        def run(engname, eobj):
            waited = {}
            for i in per_eng[engname]:
                o = ops[i]
                need = {}
                for d in o["deps"]:
                    od = ops[d]
                    if od["fn"] is None:
                        continue
                    if (not od["dma"]) and od["eng"] == "pe" and engname == "pe":
                        continue
                    s, lv, key = local(od)
                    if need.get(key, (None, -1))[1] < lv:
                        need[key] = (s, lv)
                for key, (s, lv) in need.items():
                    if waited.get(key, -1) >= lv:
                        continue
                    eobj.wait_ge(s, lv)
                    waited[key] = lv
                if o["fn"] is None:
                    continue
                ins = o["fn"](eobj)
                if o["dma"]:
                    ins.then_inc(local(o)[0], 16)
                elif o["signal"]:
                    ins.then_inc(local(o)[0], 1)

        with nc.Block() as block:
            @block.tensor
            def _(e):
                run("pe", e)

            @block.scalar
            def _(e):
                run("act", e)

            @block.vector
            def _(e):
                run("dve", e)

            @block.gpsimd
            def _(e):
                run("pool", e)

            @block.sync
            def _(e):
                run("sp", e)


def _bf(a):
    return np.ascontiguousarray(a.astype(np.float32)).astype(ml_dtypes.bfloat16)


def _consts():
    c = {}
    c["ident32"] = np.eye(128, dtype=np.float32)
    c["identb"] = _bf(np.eye(128))
    nf = 32
    inv = 10000.0 ** (-np.arange(nf, dtype=np.float32) / nf)
    pos = np.arange(SEQ)
    row = (pos // 64).astype(np.float32)
    col = (pos % 64).astype(np.float32)
    ang = np.concatenate([row[:, None] * inv[None], col[:, None] * inv[None]], axis=-1).astype(np.float32)
    cos = np.cos(ang).astype(np.float32)
    sin = np.sin(ang).astype(np.float32)
    cos = np.concatenate([np.ones((CTX, 64), np.float32), cos], 0)
    sin = np.concatenate([np.zeros((CTX, 64), np.float32), sin], 0)
    ks = np.float32(128.0 ** -0.5)
    tab = np.stack([cos, sin, -sin, cos * ks, sin * ks, -sin * ks], axis=1)
    c["rope"] = np.ascontiguousarray(tab.reshape(NT, 128, 6, 64)).astype(np.float32)
    j = np.arange(128)[:, None].astype(np.float32)
    i = np.arange(128)[None, :].astype(np.float32)
    dec = np.zeros((128, 4, 128), np.float32)
    dec[:, 0, :] = np.maximum(i - j, 0)
    dec[:, 1, :] = np.maximum(j - i, 0)
    dec[:, 2, :] = np.broadcast_to(i + 1.0, (128, 128))
    dec[:, 3, :] = np.broadcast_to(128.0 - i, (128, 128))
    c["dec"] = dec
    p = np.arange(128, dtype=np.float32)
    c["pcol"] = np.stack([127.0 - p, p], axis=1).astype(np.float32)
    GA = np.zeros((128, 32, 128), np.float64)
    b = np.arange(64)[:, None]
    cc = np.arange(64)[None, :]
    for ah in range(2):
        for al in range(32):
            a = al + 32 * ah
            th = 2 * np.pi * (b * cc / 64.0 + a * cc / 4096.0)
            GA[ah * 64:(ah + 1) * 64, al, 0:64] = np.cos(th) / 64.0
            GA[ah * 64:(ah + 1) * 64, al, 64:128] = -np.sin(th) / 64.0
    c["GA"] = _bf(GA)
    a = np.arange(64)[:, None]
    d = np.arange(64)[None, :]
    ph = 2 * np.pi * a * d / 64.0
    H = np.zeros((128, 128), np.float64)
    H[0:64, 0:64] = np.cos(ph)
    H[64:128, 0:64] = np.sin(ph)
    H[0:64, 64:128] = np.sin(ph)
    H[64:128, 64:128] = -np.cos(ph)
    c["H"] = _bf(H)
    n = np.arange(256)[:, None]
    k = np.arange(256)[None, :]
    th = 2 * np.pi * n * k / 256.0
    D256 = np.concatenate([np.cos(th) / 16.0, np.sin(th) / 16.0], axis=1)
    c["D256"] = _bf(D256.reshape(2, 128, 512).transpose(1, 0, 2))
    CS = np.concatenate([np.cos(th) / 16.0, -np.sin(th) / 16.0], axis=1)
    c["CS"] = _bf(CS.reshape(2, 128, 512).transpose(1, 0, 2))
    return c


def build(n_layers=DEPTH, debug=False):
    nc = bass.Bass("TRN2", target_bir_lowering=False)
    P = Prog(nc)

    def din(name, shape, dt=F32):
        return nc.dram_tensor(name, list(shape), dt, kind="ExternalInput").ap()

    def dscr(name, shape, dt):
        kind = "ExternalOutput" if (debug and name in ("Q", "K", "V", "Y", "XS", "GA", "U")) else "Internal"
        return nc.dram_tensor("s_" + name, list(shape), dt, kind=kind).ap()

    x_in = din("x", [SEQ, D])
    ctx_in = din("ctx", [CTX, D])
    cvT_in = din("cvT", [128, 16, 2])
    ada_w = din("ada_w", [DEPTH, D, 3 * D])
    ada_b_in = din("ada_b", [DEPTH, 3 * D])
    norm_wT = din("norm_wT", [DEPTH, 128, 16])
    ev_w_in = din("ev_w_in", [2, D, EVEN_IN])
    decay_in = din("decay", [2, 16])
    fno_w = din("fno_w", [2, 8, 256, 256])
    ev_w_out = din("ev_w_out", [2, 2 * D, D])
    od_w_in = din("od_w_in", [2, D, ODD_IN])
    conv_wT = din("conv_wT", [2, 128, 32, 3])
    od_w_out = din("od_w_out", [2, 2 * D, D])
    fnw_in = din("fnw", [1, D])
    c_ident32 = din("ident32", [128, 128])
    c_identb = din("identb", [128, 128], BF16)
    c_rope = din("rope", [NT, 128, 6, 64])
    c_dec = din("dec", [128, 4, 128])
    c_pcol = din("pcol", [128, 2])
    c_GA = din("GA", [128, 32, 128], BF16)
    c_H = din("H", [128, 128], BF16)
    c_D256 = din("D256", [128, 2, 512], BF16)
    c_CS = din("CS", [128, 2, 512], BF16)
    out_d = nc.dram_tensor("out", [SEQ, D], F32, kind="ExternalOutput").ap()

    XS = dscr("XS", [NT * 128, D], F32)
    AWb = [dscr(f"AWb{l}", [D, 3 * D], BF16) for l in range(DEPTH)]
    WIe = [dscr(f"WIe{l}", [D, EVEN_IN], BF16) for l in range(2)]
    WOe = [dscr(f"WOe{l}", [2 * D, D], BF16) for l in range(2)]
    WId = [dscr(f"WId{l}", [D, ODD_IN], BF16) for l in range(2)]
    WOd = [dscr(f"WOd{l}", [2 * D, D], BF16) for l in range(2)]
    Qs = dscr("Q", [NT, 128, 1024], BF16)
    KTs = dscr("KT", [NT, 128, 1024], BF16)
    KZs = [dscr(f"KZ{d}", [NT * 128, 1024], BF16) for d in range(2)]
    QXs = [dscr(f"QX{d}", [NT, 128, 1024], BF16) for d in range(2)]
    Vs = dscr("V", [NT * 128, 2048], BF16)
    GAs = dscr("GA", [NT * 128, 2048], F32)
    GBs = dscr("GB", [NT * 128, 2048], F32)
    Us = dscr("U", [NT * 128, 2048], BF16)
    Ys = dscr("Y", [NT * 128, 4096], BF16)
    SBS = dscr("SBS", [2, 2, NT, 128, 1024], BF16)
    ZS = dscr("ZS", [2, 2, 64, 64, 256], BF16)
    GSC = dscr("GSC", [DEPTH, 2, D], F32)

    pbank = []
    for i in range(8):
        t = nc.alloc_psum_tensor(f"pb{i}", [128, 512], F32)
        pbank.append(TB(t, f"pb{i}"))

    def pbf(i):
        return pbank[i].t[:, :].bitcast(BF16)

    ident32 = P.persist("ident32", [128, 128], F32)
    identb = P.persist("identb", [128, 128], BF16)
    modS = P.persist("modS", [128, DEPTH, 48, 2], F32)
    AT = P.persist("AT", [128, DEPTH, 2, 16], F32)
    gate = [P.persist(f"gate{r}", [128, D], F32) for r in range(2)]
    small = P.persist("small", [128, 64], F32)

    P.load(ident32, ident32[:, :], c_ident32)
    P.load(identb, identb[:, :], c_identb)

    xsb = Buf("XS")

    def x_rows(L, t):
        if L == 0:
            return ctx_in[t * 128:(t + 1) * 128, :] if t < 2 else x_in[(t - 2) * 128:(t - 1) * 128, :]
        return XS[t * 128:(t + 1) * 128, :]

    wbuf = {}
    bgq = []

    def queue_cast(key, dst, src, ncols, step):
        for c0 in range(0, ncols, step):
            bf_ = Buf(f"{key}_{c0}")
            wbuf[(key, c0 // step)] = bf_
            bgq.append((dst[:, c0:c0 + step], src[:, c0:c0 + step], bf_))

    for L in range(n_layers):
        j = L // 2
        queue_cast(("ada", L), AWb[L], ada_w[L], 3 * D, 3 * D)
        if L % 2 == 0:
            queue_cast(("wie", j), WIe[j], ev_w_in[j], EVEN_IN, 2048)
            queue_cast(("woe", j), WOe[j], ev_w_out[j], D, D)
        else:
            queue_cast(("wid", j), WId[j], od_w_in[j], ODD_IN, 2048)
            queue_cast(("wod", j), WOd[j], od_w_out[j], D, D)

    def bg_step(n=1):
        for _ in range(n):
            if bgq:
                dst, src, bf_ = bgq.pop(0)
                P.bgdma(dst, src, [], [bf_], bf_)

    bg_step(7)

    scT = P.persist("scT", [128, 16, 2], F32)
    scTb = P.persist("scTb", [128, 16, 2], BF16)
    nwT = P.persist("nwT", [128, DEPTH, 16], F32)
    P.load(scT, scT[:, :, :], cvT_in)
    P.load(nwT, nwT[:, :, :], norm_wT.rearrange("l p b -> p l b"))
    P.act(scTb[:, :, :], scT[:, :, :], AF.Silu, [scT], [scTb])
    gsc_b = Buf("GSC")

    def mod_phase(L, keep=None):
        modrow = P.tile("modrow", [2, 3 * D], F32)
        abrow = P.tile("abrow", [2, 3 * D], F32)
        ablk = [P.tile(f"ablk{k}", [128, 16, 512], BF16) for k in range(2)]
        P.load(abrow, abrow[:, :], ada_b_in[L].partition_broadcast(2))
        awv = AWb[L].rearrange("(kc p) n -> p kc n", p=128)
        for blk in range(12):
            a = ablk[blk % 2]
            P.load(a, a[:, 0:8, :], awv[:, 0:8, blk * 512:(blk + 1) * 512], reads=[wbuf[(("ada", L), 0)]])
            P.load(a, a[:, 8:16, :], awv[:, 8:16, blk * 512:(blk + 1) * 512], reads=[wbuf[(("ada", L), 0)]])
            pr = pbank[blk % 2]
            for kc in range(16):
                P.mm(pr.t[0:2, :], scTb[:, kc, :], a[:, kc, :], kc == 0, kc == 15, [scTb, a], [pr])
            P.tt("dve", modrow[:, blk * 512:(blk + 1) * 512], pr.t[0:2, :], abrow[:, blk * 512:(blk + 1) * 512],
                 ALU.add, [pr, abrow], [modrow], disjoint=True)
        pm = pbank[2]
        for blk in range(32):
            P.tr(pm.t[:, blk * 2:blk * 2 + 2], modrow[0:2, blk * 128:(blk + 1) * 128], ident32[0:2, 0:2],
                 [modrow, ident32], [pm])
        P.copy("dve", modS[:, L, 0:32, :], pm.t[:, 0:64].rearrange("p (b r) -> p b r", r=2), [pm], [modS],
               disjoint=True)
        for r in range(2):
            P.stt("dve", AT[:, L, r, :], modS[:, L, 16:32, r], 1.0, nwT[:, L, :], ALU.add, ALU.mult,
                  [modS, nwT], [AT], disjoint=True)
        P.store(GSC[L], modrow, modrow[0:2, 2 * D:3 * D], writes=[gsc_b])
        for r in range(2):
            P.load(gate[r], gate[r][:, :], GSC[L, r].partition_broadcast(128), reads=[gsc_b])
        if keep is not None:
            keep()
        P.barrier(keep=None if keep is None else keep.mark)

    def norm_tile(t, xt, xn, junk, L, r):
        ss = small[:, 0:1]
        if junk is None:
            junk = xn
        P.act(junk[:, :], xt[:, :], AF.Square, [xt], [junk, small], accum=ss)
        P.act(small[:, 1:2], ss, AF.Sqrt, [small], [small], bias=EPS, scale=1.0 / D)
        P.recip(small[:, 2:3], small[:, 1:2], [small], [small])
        P.act(xn[:, :], xt[:, :], AF.Copy, [xt, small], [xn], scale=small[:, 2:3])

    def hT_tile(L, t, j, hT, xt, xn, junk, pb_ids):
        r = 1 if t < 2 else 0
        P.load(xt, xt[:, :], x_rows(L, t))
        norm_tile(t, xt, xn, junk, L, r)
        for q in range(4):
            pb = pbank[pb_ids[q % len(pb_ids)]]
            for s_ in range(4):
                kc = q * 4 + s_
                P.tr(pb.t[:, s_ * 128:(s_ + 1) * 128], xn[:, kc * 128:(kc + 1) * 128], ident32[:, :],
                     [xn, ident32], [pb])
            for s_ in range(4):
                kc = q * 4 + s_
                P.act(hT[:, kc, j * 128:(j + 1) * 128], pb.t[:, s_ * 128:(s_ + 1) * 128], AF.Identity,
                      [pb, AT, modS], [hT], bias=modS[:, L, kc, r:r + 1], scale=AT[:, L, r, kc:kc + 1],
                      disjoint=True)

    def make_hT(L, tiles, hT, xts, xns, junk, pb_ids):
        for j, t in enumerate(tiles):
            hT_tile(L, t, j, hT, xts[j % len(xts)], xns[j % len(xns)], junk, pb_ids)

    fnw_junk = [None]
    groups_all = [[0, 1]] + [[2 + 4 * g + k for k in range(4)] for g in range(8)]

    def even_layer(L):
        j = L // 2
        last_even = (L == 2)
        lg = P.tile("lg", [128, 16], F32)
        MT = P.tile("MT", [128, 8, 128], F32)
        XF = P.tile("XF", [128, 8, 128], F32)
        XB = P.tile("XB", [128, 8, 128], F32)
        ZZ = P.tile("ZZ", [128, 4, 8], F32)
        tmark = P.sb_ptr

        def tables():
            dec = P.tile("dec", [128, 4, 128], F32)
            pcol = P.tile("pcol", [128, 2], F32)
            tmpm = P.tile("tmpm", [128, 2, 128], F32)
            P.load(dec, dec[:, :, :], c_dec)
            P.load(pcol, pcol[:, :], c_pcol)
            P.load(lg, lg[:, :], decay_in[j].partition_broadcast(128))
            P.act(lg[:, :], lg[:, :], AF.Exp, [lg], [lg])
            P.ts("dve", lg[:, :], lg[:, :], -1.0, None, ALU.mult, None, [lg], [lg])
            for h in range(8):
                P.ts("dve", tmpm[:, 0, :], dec[:, 0, :], lg[:, h:h + 1], None, ALU.mult, None, [dec, lg], [tmpm])
                P.stt("dve", tmpm[:, 1, :], dec[:, 1, :], lg[:, 8 + h:9 + h], tmpm[:, 0, :], ALU.mult, ALU.add,
                      [dec, lg, tmpm], [tmpm])
                P.act(MT[:, h, :], tmpm[:, 1, :], AF.Exp, [tmpm], [MT], disjoint=True)
                P.act(XF[:, h, :], dec[:, 2, :], AF.Exp, [dec, lg], [XF], scale=lg[:, h:h + 1], disjoint=True)
                P.act(XB[:, h, :], dec[:, 3, :], AF.Exp, [dec, lg], [XB], scale=lg[:, 8 + h:9 + h], disjoint=True)
            P.act(ZZ[:, 0, :], lg[:, 0:8], AF.Exp, [lg, pcol], [ZZ], scale=pcol[:, 0:1], disjoint=True)
            P.act(ZZ[:, 1, :], lg[:, 8:16], AF.Exp, [lg, pcol], [ZZ], scale=pcol[:, 1:2], disjoint=True)
            P.act(ZZ[:, 2, :], lg[:, 0:8], AF.Exp, [lg], [ZZ], scale=128.0, disjoint=True)
            P.act(ZZ[:, 3, :], lg[:, 8:16], AF.Exp, [lg], [ZZ], scale=128.0, disjoint=True)

        tables.mark = tmark
        mod_phase(L, keep=tables)
        if STOP_AFTER == 'M':
            return

        xts = [P.tile(f"xt{k}", [128, D], F32) for k in range(2)]
        xn = P.tile("xn0", [128, D], F32)
        hTs = [P.tile(f"hT{k}", [128, 16, 1024], BF16) for k in range(2)]
        wbs = [P.tile(f"wb{k}", [128, 16, 512], BF16) for k in range(3)]
        rp = P.tile("rope", [128, 8, 6, 64], F32)
        ra = [P.tile(f"ra{k}", [128, 512], F32) for k in range(2)]
        rb = [P.tile(f"rb{k}", [128, 512], F32) for k in range(2)]
        qbf = [P.tile(f"qbf{k}", [128, 512], BF16) for k in range(2)]
        qts = [P.tile(f"qts{k}", [128, 512], BF16) for k in range(2)]
        qxs = [[P.tile(f"qx{d}{k}", [128, 512], BF16) for k in range(2)] for d in range(2)]
        ebf = [P.tile(f"ebf{k}", [128, 512], BF16) for k in range(2)]
        e32 = [P.tile(f"e32{k}", [128, 512], F32) for k in range(2)]
        nw = 0
        ne = 0
        nx = [0]
        groups_p = [[0, 1]] + [[2 + 8 * g + k for k in range(8)] for g in range(4)]
        pending = []

        def prep_tile(gi, jj):
            t = groups_p[gi][jj]
            hT_tile(L, t, jj, hTs[gi % 2], xts[nx[0] % 2], xn, None, [0, 1])
            nx[0] += 1

        def qk_tail(nb, jt, t, k, tag):
            isk = nb >= 2
            hg = nb % 2
            hs = slice(hg * 4, hg * 4 + 4)
            rows = slice(t * 128, (t + 1) * 128)
            qb_, qt_ = qbf[k], qts[k]
            pt = pbank[5 + (tag % 2)]
            ptv = pbf(5 + (tag % 2))
            for h in range(4):
                P.tr(ptv[:, h * 128:(h + 1) * 128], qb_[:, h * 128:(h + 1) * 128], identb[:, :],
                     [qb_, identb], [pt])
            P.copy("act", qt_[:, :], ptv[:, 0:512], [pt], [qt_])
            dst = (KTs if isk else Qs)[t][:, hg * 512:(hg + 1) * 512]
            P.store(dst, qt_, qt_[:, :])
            for d in range(2):
                qx = qxs[d][k]
                if isk:
                    P.tt("dve", qx[:, :].rearrange("p (h c) -> p h c", c=128),
                         qb_[:, :].rearrange("p (h c) -> p h c", c=128),
                         ZZ[:, d, hs].unsqueeze(2).to_broadcast([128, 4, 128]), ALU.mult, [qb_, ZZ], [qx])
                    P.store(KZs[d][rows, hg * 512:(hg + 1) * 512], qx, qx[:, :])
                else:
                    tab = XF if d == 0 else XB
                    P.tt("dve", qx[:, :], qt_[:, :], tab[:, hs, :].rearrange("p h c -> p (h c)"), ALU.mult,
                         [qt_, tab], [qx])
                    P.store(QXs[d][t][:, hg * 512:(hg + 1) * 512], qx, qx[:, :])

        for jj in range(len(groups_p[0])):
            prep_tile(0, jj)
        for gi, tiles in enumerate(groups_p):
            hT = hTs[gi % 2]
            for jt, t in enumerate(tiles):
                P.load(rp, rp[:, jt, :, :], c_rope[t])
            for nb in range(20):
                wb = wbs[nw % 3]
                nw += 1
                wsrc = WIe[j].rearrange("(kc p) n -> p kc n", p=128)[:, :, nb * 512:(nb + 1) * 512]
                wdep = [wbuf[(("wie", j), nb // 4)]]
                P.load(wb, wb[:, 0:8, :], wsrc[:, 0:8, :], reads=wdep)
                P.load(wb, wb[:, 8:16, :], wsrc[:, 8:16, :], reads=wdep)
                for jt, t in enumerate(tiles):
                    pa = pbank[2 + (ne % 3)]
                    for kc in range(16):
                        P.mm(pa.t[:, :], hT[:, kc, jt * 128:(jt + 1) * 128], wb[:, kc, :], kc == 0, kc == 15,
                             [hT, wb], [pa])
                    while pending:
                        pending.pop(0)()
                    if jt == 0 and gi + 1 < len(groups_p) and nb >= 2 and (nb - 2) % 2 == 0:
                        jj = (nb - 2) // 2
                        if jj < len(groups_p[gi + 1]):
                            prep_tile(gi + 1, jj)
                    k = ne % 2
                    ne += 1
                    rows = slice(t * 128, (t + 1) * 128)
                    if nb < 4:
                        isk = nb >= 2
                        tb = 3 if isk else 0
                        A, Bm, qb_ = ra[k], rb[k], qbf[k]
                        pv = pa.t[:, :].rearrange("p (g c) -> p g c", c=64)
                        P.tt("dve", A[:, :].rearrange("p (g c) -> p g c", c=64), pv,
                             rp[:, jt, tb, :].unsqueeze(1).to_broadcast([128, 8, 64]), ALU.mult, [pa, rp], [A])
                        p4 = pa.t[:, :].rearrange("p (h two c) -> p h two c", two=2, c=64)
                        b4 = Bm[:, :].rearrange("p (h two c) -> p h two c", two=2, c=64)
                        P.tt("dve", b4[:, :, 0, :], p4[:, :, 1, :],
                             rp[:, jt, tb + 2, :].unsqueeze(1).to_broadcast([128, 4, 64]), ALU.mult, [pa, rp], [Bm])
                        P.tt("dve", b4[:, :, 1, :], p4[:, :, 0, :],
                             rp[:, jt, tb + 1, :].unsqueeze(1).to_broadcast([128, 4, 64]), ALU.mult, [pa, rp], [Bm],
                             disjoint=True)
                        P.tt("dve", qb_[:, :], A[:, :], Bm[:, :], ALU.add, [A, Bm], [qb_])
                        pending.append(lambda nb=nb, jt=jt, t=t, k=k, tag=ne: qk_tail(nb, jt, t, k, tag))
                    elif nb < 8 or (12 <= nb < 16):
                        e = ebf[k]
                        P.copy("act", e[:, :], pa.t[:, :], [pa], [e])
                        if nb < 8:
                            P.store(Vs[rows, (nb - 4) * 512:(nb - 3) * 512], e, e[:, :])
                        else:
                            P.store(Us[rows, (nb - 12) * 512:(nb - 11) * 512], e, e[:, :])
                    else:
                        e = e32[k]
                        P.act(e[:, :], pa.t[:, :], AF.Silu, [pa], [e])
                        if nb < 12:
                            P.store(GAs[rows, (nb - 8) * 512:(nb - 7) * 512], e, e[:, :])
                        else:
                            P.store(GBs[rows, (nb - 16) * 512:(nb - 15) * 512], e, e[:, :])
        while pending:
            pending.pop(0)()
        P.barrier(keep=tmark)
        if STOP_AFTER == 'P':
            return

        order_f = list(range(NT))
        order_b = [1, 0] + list(range(NT - 1, 1, -1))
        snap_b = Buf("SNAP")
        chains = []
        for hg in range(2):
            for dr in range(2):
                ci = hg * 2 + dr
                chains.append(dict(
                    hg=hg, dr=dr, order=order_f if dr == 0 else order_b,
                    S32=P.tile(f"S32_{ci}", [128, 1024], F32),
                    Sbf=[P.tile(f"Sbf{ci}_{k}", [128, 1024], BF16) for k in range(2)],
                    kz=[P.tile(f"kz{ci}_{k}", [128, 512], BF16) for k in range(3)],
                    vt=[P.tile(f"vt{ci}_{k}", [128, 1024], BF16) for k in range(3)],
                    pb=(pbank[2 * ci], pbank[2 * ci + 1])))
        for c in chains:
            P.memset("dve", c["S32"][:, :], 0.0, [c["S32"]])
            P.memset("dve", c["Sbf"][0][:, :], 0.0, [c["Sbf"][0]])
        for n in range(NT):
            for c in chains:
                hg, dr = c["hg"], c["dr"]
                t = c["order"][n]
                k = n % 3
                cur = c["Sbf"][n % 2]
                nxt = c["Sbf"][(n + 1) % 2]
                rows = slice(t * 128, (t + 1) * 128)
                P.store(SBS[dr, hg, t], cur, cur[:, :], writes=[snap_b])
                if n == NT - 1:
                    continue
                kz, vt, S32 = c["kz"][k], c["vt"][k], c["S32"]
                P.load(kz, kz[:, :], KZs[dr][rows, hg * 512:(hg + 1) * 512])
                P.load(vt, vt[:, :], Vs[rows, hg * 1024:(hg + 1) * 1024])
                for half in range(2):
                    pi = c["pb"][half]
                    for hh in range(2):
                        h = half * 2 + hh
                        P.mm(pi.t[:, hh * 256:(hh + 1) * 256], kz[:, h * 128:(h + 1) * 128],
                             vt[:, h * 256:(h + 1) * 256], True, True, [kz, vt], [pi])
                for half in range(2):
                    pi = c["pb"][half]
                    for hh in range(2):
                        h = half * 2 + hh
                        P.stt("dve", S32[:, h * 256:(h + 1) * 256], S32[:, h * 256:(h + 1) * 256],
                              ZZ[:, 2 + dr, hg * 4 + h:hg * 4 + h + 1], pi.t[:, hh * 256:(hh + 1) * 256],
                              ALU.mult, ALU.add, [S32, ZZ, pi], [S32], disjoint=(h > 0))
                P.copy("act", nxt[:, :], S32[:, :], [S32], [nxt])
        P.barrier(keep=tmark)
        if STOP_AFTER == 'RS':
            return

        NI = 4
        qTt = [P.tile(f"qTt{k}", [128, 512], BF16) for k in range(NI)]
        kTt = [P.tile(f"kTt{k}", [128, 512], BF16) for k in range(NI)]
        qft = [P.tile(f"qft{k}", [128, 512], BF16) for k in range(NI)]
        qbt = [P.tile(f"qbt{k}", [128, 512], BF16) for k in range(NI)]
        vt = [P.tile(f"vt{k}", [128, 1024], BF16) for k in range(NI)]
        sga = [P.tile(f"sga{k}", [128, 1024], F32) for k in range(NI)]
        sfl = [P.tile(f"sfl{k}", [128, 1024], BF16) for k in range(NI)]
        sbl = [P.tile(f"sbl{k}", [128, 1024], BF16) for k in range(NI)]
        sdt = [P.tile(f"sdt{k}", [128, 512], BF16) for k in range(3)]
        yt = [P.tile(f"yt{k}", [128, 1024], BF16) for k in range(2)]
        rs = [P.tile(f"rs{k}", [128, 16], F32) for k in range(2)]
        jks = [[P.tile(f"jk{a}{h}", [128, 256], BF16) for h in range(4)] for a in range(2)]
        its = [(t, hg) for t in range(NT) for hg in range(2)]

        def stA(i):
            t, hg = its[i]
            hs = slice(hg * 4, hg * 4 + 4)
            k = i % NI
            rows = slice(t * 128, (t + 1) * 128)
            P.load(qTt[k], qTt[k][:, :], Qs[t][:, hg * 512:(hg + 1) * 512])
            P.load(kTt[k], kTt[k][:, :], KTs[t][:, hg * 512:(hg + 1) * 512])
            P.load(qft[k], qft[k][:, :], QXs[0][t][:, hg * 512:(hg + 1) * 512])
            P.load(qbt[k], qbt[k][:, :], QXs[1][t][:, hg * 512:(hg + 1) * 512])
            P.load(vt[k], vt[k][:, :], Vs[rows, hg * 1024:(hg + 1) * 1024])
            P.load(sga[k], sga[k][:, :], GAs[rows, hg * 1024:(hg + 1) * 1024])
            P.load(sfl[k], sfl[k][:, :], SBS[0, hg, t])
            P.load(sbl[k], sbl[k][:, :], SBS[1, hg, t])
            pst = pbank[i % 2]
            for h in range(4):
                P.mm(pst.t[:, h * 128:(h + 1) * 128], kTt[k][:, h * 128:(h + 1) * 128],
                     qTt[k][:, h * 128:(h + 1) * 128], True, True, [kTt[k], qTt[k]], [pst])
            P.tt("dve", sdt[i % 3][:, :], pst.t[:, :], MT[:, hs, :].rearrange("p h c -> p (h c)"), ALU.mult,
                 [pst, MT], [sdt[i % 3]])

        def stB(i):
            k = i % NI
            pos = (pbank[2 + 2 * (i % 2)], pbank[3 + 2 * (i % 2)])
            sd = sdt[i % 3]
            r_ = rs[i % 2]
            for half in range(2):
                po = pos[half]
                for hh in range(2):
                    h = half * 2 + hh
                    o_ap = po.t[:, hh * 256:(hh + 1) * 256]
                    P.mm(o_ap, sd[:, h * 128:(h + 1) * 128], vt[k][:, h * 256:(h + 1) * 256], True, False,
                         [sd, vt[k]], [po])
                    P.mm(o_ap, qft[k][:, h * 128:(h + 1) * 128], sfl[k][:, h * 256:(h + 1) * 256], False, False,
                         [qft[k], sfl[k]], [po])
                    P.mm(o_ap, qbt[k][:, h * 128:(h + 1) * 128], sbl[k][:, h * 256:(h + 1) * 256], False, True,
                         [qbt[k], sbl[k]], [po])
                for hh in range(2):
                    h = half * 2 + hh
                    jk = jks[i % 2][h]
                    P.act(jk[:, :], po.t[:, hh * 256:(hh + 1) * 256], AF.Square, [po], [jk], accum=r_[:, h:h + 1])
                    P.ops[-1]["deps"]
                    r_.b.writers.append(len(P.ops) - 1)
            P.act(r_[:, 4:8], r_[:, 0:4], AF.Sqrt, [r_], [r_], bias=EPS, scale=1.0 / 256.0)
            P.recip(r_[:, 8:12], r_[:, 4:8], [r_], [r_])

        def stC(i):
            t, hg = its[i]
            k = i % NI
            rows = slice(t * 128, (t + 1) * 128)
            pos = (pbank[2 + 2 * (i % 2)], pbank[3 + 2 * (i % 2)])
            r_ = rs[i % 2]
            y_ = yt[i % 2]
            for half in range(2):
                po = pos[half]
                for hh in range(2):
                    h = half * 2 + hh
                    P.stt("dve", y_[:, h * 256:(h + 1) * 256], po.t[:, hh * 256:(hh + 1) * 256],
                          r_[:, 8 + h:9 + h], sga[k][:, h * 256:(h + 1) * 256], ALU.mult, ALU.mult,
                          [po, r_, sga[k]], [y_], disjoint=(h > 0))
            P.store(Ys[rows, hg * 1024:(hg + 1) * 1024], y_, y_[:, :])

        NIT = len(its)
        for s_ in range(NIT + 2):
            if s_ < NIT:
                stA(s_)
            if 0 <= s_ - 1 < NIT:
                stB(s_ - 1)
            if 0 <= s_ - 2 < NIT:
                stC(s_ - 2)
        P.barrier()
        if STOP_AFTER == 'RO':
            return

        GAc = P.tile("GAc", [128, 32, 128], BF16)
        Hc = P.tile("Hc", [128, 128], BF16)
        D256 = P.tile("D256", [128, 2, 512], BF16)
        CSc = P.tile("CSc", [128, 2, 512], BF16)
        WCS = P.tile("WCS", [128, 8, 2, 2, 256], BF16)
        gbt = [P.tile(f"gbt{k}", [128, 4, 256], F32) for k in range(2)]
        ybt = [P.tile(f"ybt{k}", [128, 4, 256], BF16) for k in range(2)]
        fmark = P.sb_ptr
        fw32 = P.tile("fw32", [128, 4, 2, 256], F32)
        fwb = P.tile("fwb", [128, 8, 2, 256], BF16)
        P.load(GAc, GAc[:, :, :], c_GA)
        P.load(Hc, Hc[:, :], c_H)
        P.load(D256, D256[:, :, :], c_D256)
        P.load(CSc, CSc[:, :, :], c_CS)
        for half in range(2):
            P.load(fw32, fw32[:, :, :, :],
                   fno_w[j, half * 4:(half + 1) * 4].rearrange("g (kc p) d -> p g kc d", p=128))
            P.copy("dve", fwb[:, half * 4:(half + 1) * 4, :, :], fw32[:, :, :, :], [fw32], [fwb], disjoint=True)
        nb_ = 0
        for g in range(8):
            for mb in range(2):
                pw = pbank[nb_ % 2]
                nb_ += 1
                for cs in range(2):
                    for kc in range(2):
                        P.mm(pw.t[:, cs * 256:(cs + 1) * 256],
                             CSc[:, kc, cs * 256 + mb * 128: cs * 256 + (mb + 1) * 128], fwb[:, g, kc, :],
                             kc == 0, kc == 1, [CSc, fwb], [pw])
                P.copy("act", WCS[:, g, mb, :, :], pw.t[:, :].rearrange("p (c d) -> p c d", d=256), [pw], [WCS],
                       disjoint=True)

        nmix = [0]

        def mixing(pq_ap_fn, tiles, g):
            for q0 in range(0, len(tiles), 4):
                sub = tiles[q0:q0 + 4]
                k = nmix[0] % 2
                nmix[0] += 1
                r0 = sub[0] * 128
                nr = len(sub)
                P.load(gbt[k], gbt[k][:, 0:nr, :],
                       GBs[r0:r0 + nr * 128, g * 256:(g + 1) * 256].rearrange("(t p) c -> p t c", p=128))
                for pr in range(0, nr, 2):
                    pm_ = pbank[6 + ((q0 // 2 + pr // 2) % 2)]
                    for u in range(2):
                        jt = q0 + pr + u
                        idx = 0
                        for kc in range(2):
                            for pq in range(2):
                                P.mm(pm_.t[:, u * 256:(u + 1) * 256], pq_ap_fn(kc, pq, jt), WCS[:, g, kc, pq, :],
                                     idx == 0, idx == 3, pq_reads + [WCS], [pm_])
                                idx += 1
                    P.tt("dve", ybt[k][:, pr:pr + 2, :], pm_.t[:, :].rearrange("p (t c) -> p t c", c=256),
                         gbt[k][:, pr:pr + 2, :], ALU.mult, [pm_, gbt[k]], [ybt[k]], disjoint=(pr > 0))
                P.store(Ys[r0:r0 + nr * 128, 2048 + g * 256: 2048 + (g + 1) * 256].rearrange("(t p) c -> p t c", p=128),
                        ybt[k], ybt[k][:, 0:nr, :])

        if not last_even:
            Uc = P.tile("Uc", [128, 2, D], BF16)
            PQc = P.tile("PQc", [128, 16, 2, 256], BF16)
            P.load(Uc, Uc[:, :, :], Us[0:CTX, :].rearrange("(t p) c -> p t c", p=128))
            for chb in range(16):
                pc = pbank[chb % 2]
                for tl in range(2):
                    P.mm(pc.t[:, :], Uc[:, tl, chb * 128:(chb + 1) * 128], D256[:, tl, :], tl == 0, tl == 1,
                         [Uc, D256], [pc])
                P.copy("act" if chb % 2 else "dve", PQc[:, chb, :, :],
                       pc.t[:, :].rearrange("p (q k) -> p q k", k=256), [pc], [PQc], disjoint=True)
            pq_reads = [PQc]
            for g in range(8):
                mixing(lambda kc, pq, jt, g=g: PQc[:, 2 * g + kc, pq, jt * 128:(jt + 1) * 128], [0, 1], g)

        P.barrier(keep=fmark)
        U2 = P.tile("U2", [128, 32, 256], BF16)
        Z1 = P.tile("Z1", [128, 64, 256], BF16)
        Z1ps = [P.tile(f"Z1p{k}", [128, 64, 256], BF16) for k in range(2)]
        PQT = P.tile("PQT", [128, 2, 2, SEQ], BF16)
        zs_b = [Buf("ZS0"), Buf("ZS1")]
        nev = [0]

        def stage1(g):
            s_ = g % 2
            ulat = Us[CTX:, g * 256:(g + 1) * 256].rearrange("(b ah al) c -> ah b al c", ah=2, al=32)
            for ah in range(2):
                for q in range(2):
                    P.load(U2, U2[ah * 64:(ah + 1) * 64, q * 16:(q + 1) * 16, :], ulat[ah][:, q * 16:(q + 1) * 16, :])
            for al in range(0, 32, 2):
                pA = pbank[0 + (al // 2) % 2]
                pB = pbank[2 + (al // 2) % 2]
                for u in range(2):
                    P.mm(pA.t[:, u * 256:(u + 1) * 256], GAc[0:64, al + u, :], U2[0:64, al + u, :], True, True,
                         [GAc, U2], [pA])
                    P.mm(pB.t[:, u * 256:(u + 1) * 256], GAc[64:128, al + u, :], U2[64:128, al + u, :], True, True,
                         [GAc, U2], [pB])
                P.copy("act", Z1[:, al:al + 2, :], pA.t[:, :].rearrange("p (a c) -> p a c", c=256), [pA], [Z1],
                       disjoint=True)
                P.copy("dve", Z1[:, 32 + al:32 + al + 2, :], pB.t[:, :].rearrange("p (a c) -> p a c", c=256), [pB],
                       [Z1], disjoint=True)
            for q in range(4):
                P.store(ZS[s_].rearrange("ri c a ch -> (ri c) a ch")[:, q * 16:(q + 1) * 16, :], Z1,
                        Z1[:, q * 16:(q + 1) * 16, :], writes=[zs_b[s_]])
            Z1p = Z1ps[s_]
            for ri in range(2):
                for q in range(4):
                    P.load(Z1p, Z1p[ri * 64:(ri + 1) * 64, q * 16:(q + 1) * 16, :],
                           ZS[s_, ri].rearrange("c a ch -> a c ch")[:, q * 16:(q + 1) * 16, :], reads=[zs_b[s_]])

        def stage2(g):
            Z1p = Z1ps[g % 2]
            for chb in range(2):
                for c0 in range(0, 64, 4):
                    ps = pbank[4 + (nev[0] % 2)]
                    for u in range(4):
                        P.mm(ps.t[:, u * 128:(u + 1) * 128], Z1p[:, c0 + u, chb * 128:(chb + 1) * 128], Hc[:, :],
                             True, True, [Z1p, Hc], [ps])
                    dst = PQT[:, chb, :, :].rearrange("p q (d c) -> p c q d", c=64)[:, c0:c0 + 4, :, :]
                    src = ps.t[:, :].rearrange("p (c q d) -> p c q d", q=2, d=64)
                    P.copy("act" if nev[0] % 2 else "dve", dst, src, [ps], [PQT], disjoint=True)
                    nev[0] += 1

        pq_reads = [PQT]
        stage1(0)
        for g in range(8):
            if g + 1 < 8:
                stage1(g + 1)
            stage2(g)
            mixing(lambda kc, pq, jt: PQT[:, kc, pq, jt * 128:(jt + 1) * 128], list(range(2, NT)), g)
        P.barrier()

        out_phase_even(L)

    def final_norm_store(xt, t, fnw_t, junk):
        if t < 2:
            return
        ss = small[:, 8:9]
        if junk is None:
            junk = fnw_junk[0]
        P.act(junk[:, :], xt[:, :], AF.Square, [xt], [junk, small], accum=ss)
        P.act(small[:, 9:10], ss, AF.Sqrt, [small], [small], bias=EPS, scale=1.0 / D)
        P.recip(small[:, 10:11], small[:, 9:10], [small], [small])
        P.stt("dve", xt[:, :], xt[:, :], small[:, 10:11], fnw_t[:, :], ALU.mult, ALU.mult, [xt, small, fnw_t], [xt])
        P.store(out_d[(t - 2) * 128:(t - 1) * 128, :], xt, xt[:, :])

    def out_phase_even(L):
        is_last = (L == DEPTH - 1)
        yts = [P.tile(f"yl{k}", [128, 4096], BF16) for k in range(2)]
        yTs = [P.tile(f"yT{k}", [128, 32, 128], BF16) for k in range(8)]
        xos = [P.tile(f"xo{k}", [128, D], F32) for k in range(5)]
        wos = [P.tile(f"wo{k}", [128, 32, 256], BF16) for k in range(2)]
        tmp = [P.tile(f"tmp{k}", [128, 256], F32) for k in range(2)]
        junk = P.tile("junko", [128, D], BF16)
        fnw_t = None
        if is_last:
            fnw_t = P.tile("fnw", [128, D], F32)
            P.load(fnw_t, fnw_t[:, :], fnw_in.rearrange("o d -> (o d)").partition_broadcast(128))
        groups = groups_all if L < 2 else groups_all[1:]
        nyl = [0]
        nysl = [0]

        def prep_y(t):
            sl = nysl[0] % 8
            nysl[0] += 1
            yl = yts[nyl[0] % 2]
            nyl[0] += 1
            P.load(yl, yl[:, :], Ys[t * 128:(t + 1) * 128, :])
            for q in range(4):
                pt = pbank[q % 2]
                ptv = pbf(q % 2)
                for s_ in range(8):
                    kc = q * 8 + s_
                    P.tr(ptv[:, s_ * 128:(s_ + 1) * 128], yl[:, kc * 128:(kc + 1) * 128], identb[:, :],
                         [yl, identb], [pt])
                P.copy("act" if q % 2 else "dve", yTs[sl][:, q * 8:(q + 1) * 8, :],
                       ptv[:, :].rearrange("p (k c) -> p k c", c=128), [pt], [yTs[sl]], disjoint=True)
            return sl

        nslot = 0
        nwo = 0
        nacc = 0
        ysl = {0: [prep_y(t) for t in groups[0]]}
        for gi, tiles in enumerate(groups):
            slots = []
            for jt, t in enumerate(tiles):
                sl = nslot % 5
                nslot += 1
                slots.append(sl)
                P.load(xos[sl], xos[sl][:, :], x_rows(L, t))
            bg_step(2 if gi < 2 else 1)
            nxt = list(groups[gi + 1]) if gi + 1 < len(groups) else []
            ysl[gi + 1] = []
            for nb in range(8):
                wo = wos[nwo % 2]
                nwo += 1
                for q in range(2):
                    P.load(wo, wo[:, q * 16:(q + 1) * 16, :],
                           WOe[L // 2].rearrange("(kc p) n -> p kc n", p=128)[:, q * 16:(q + 1) * 16, nb * 256:(nb + 1) * 256],
                           reads=[wbuf[(("woe", L // 2), 0)]])
                for jt, t in enumerate(tiles):
                    sl = slots[jt]
                    r = 1 if t < 2 else 0
                    pa = pbank[2 + (nacc % 4)]
                    tm = tmp[nacc % 2]
                    nacc += 1
                    ys = yTs[ysl[gi][jt]]
                    for kc in range(32):
                        P.mm(pa.t[:, 0:256], ys[:, kc, :], wo[:, kc, :], kc == 0, kc == 31, [ys, wo], [pa])
                    cs = slice(nb * 256, (nb + 1) * 256)
                    P.tt("dve", tm[:, :], pa.t[:, 0:256], gate[r][:, cs], ALU.mult, [pa, gate[r]], [tm])
                    P.tt("dve", xos[sl][:, cs], xos[sl][:, cs], tm[:, :], ALU.add, [xos[sl], tm], [xos[sl]])
                if nb % 2 == 1 and nxt:
                    ysl[gi + 1].append(prep_y(nxt.pop(0)))
            while nxt:
                ysl[gi + 1].append(prep_y(nxt.pop(0)))
            for jt, t in enumerate(tiles):
                sl = slots[jt]
                if is_last:
                    final_norm_store(xos[sl], t, fnw_t, junk)
                else:
                    P.store(XS[t * 128:(t + 1) * 128, :], xos[sl], xos[sl][:, :])
        P.barrier()

    def odd_layer(L):
        j = L // 2
        is_last = (L == DEPTH - 1)

        mod_phase(L)

        cw = P.tile("cw", [128, 32, 3], F32)
        P.load(cw, cw[:, :, :], conv_wT[j])
        xns = [P.tile("xn0", [128, D], F32)]
        junk = None
        fnw_junk[0] = xns[0]
        hTs = [P.tile("hT0", [128, 16, 512], BF16)]
        wds = [P.tile(f"wd{k}", [128, 4, 16, 128], BF16) for k in range(2)]
        yTg = [P.tile("yTg0", [128, 32, 512], BF16)]
        cgs = [P.tile("cgs0", [128, 512], F32)]
        zt = [P.tile("zt0", [128, 512], F32)]
        ct = [P.tile("ct0", [128, 512], F32)]
        sg = [P.tile("sg0", [128, 512], F32)]
        y1 = [P.tile("y10", [128, 512], F32)]
        xos = [P.tile(f"xo{k}", [128, D], F32) for k in range(4)]
        xtp = P.tile("xtp", [128, D], F32)
        wos = [P.tile(f"wo{k}", [128, 32, 256], BF16) for k in range(2)]
        tmp = [P.tile(f"tmp{k}", [128, 256], F32) for k in range(2)]
        fnw_t = None
        if is_last:
            fnw_t = P.tile("fnw", [128, D], F32)
            P.load(fnw_t, fnw_t[:, :], fnw_in.rearrange("o d -> (o d)").partition_broadcast(128))
        groups = groups_all if L < 3 else groups_all[1:]
        nwd = 0
        nslot = 0
        nwo = 0
        nacc = 0
        for gi, tiles in enumerate(groups):
            hT = hTs[0]
            yT = yTg[0]
            ntok = len(tiles) * 128
            rowlen = 64 if tiles[0] >= 2 else ntok
            slots = list(range(len(tiles)))
            if gi == 0:
                for jj, t in enumerate(tiles):
                    hT_tile(L, t, jj, hT, xtp, xns[0], None, [0, 1])
            for jb in range(32):
                wd = wds[nwd % 2]
                k = 0
                nwd += 1
                wsrc = WId[j].rearrange("(kc p) (pt n) -> p pt kc n", p=128, pt=4)[:, :, :, jb * 128:(jb + 1) * 128]
                for pt_ in range(4):
                    P.load(wd, wd[:, pt_, :, :], wsrc[:, pt_, :, :], reads=[wbuf[(("wid", j), pt_ * 2 + jb // 16)]])
                if jb == 16:
                    bg_step()
                pp = [pbank[2 + i] for i in range(4)] if jb % 2 == 0 else [pbank[6], pbank[7], pbank[0], pbank[1]]
                for part in range(4):
                    for kc in range(16):
                        P.mm(pp[part].t[:, 0:ntok], wd[:, part, kc, :], hT[:, kc, 0:ntok], kc == 0, kc == 15,
                             [wd, hT], [pp[part]])
                pbg, pcg, pxt, pg = pp
                P.copy("act", cgs[k][:, 0:ntok], pcg.t[:, 0:ntok], [pcg], [cgs[k]])
                P.tt("dve", zt[k][:, 0:ntok], pxt.t[:, 0:ntok], cgs[k][:, 0:ntok], ALU.mult, [pxt, cgs[k]], [zt[k]])
                P.act(ct[k][:, 0:ntok], zt[k][:, 0:ntok], AF.Copy, [zt[k], cw], [ct[k]], scale=cw[:, jb, 1:2])
                z3 = zt[k][:, 0:ntok].rearrange("p (r c) -> p r c", c=rowlen)
                c3 = ct[k][:, 0:ntok].rearrange("p (r c) -> p r c", c=rowlen)
                P.stt("dve", c3[:, :, 1:rowlen], z3[:, :, 0:rowlen - 1], cw[:, jb, 0:1], c3[:, :, 1:rowlen],
                      ALU.mult, ALU.add, [zt[k], cw, ct[k]], [ct[k]])
                P.stt("dve", c3[:, :, 0:rowlen - 1], z3[:, :, 1:rowlen], cw[:, jb, 2:3], c3[:, :, 0:rowlen - 1],
                      ALU.mult, ALU.add, [zt[k], cw, ct[k]], [ct[k]])
                P.act(sg[k][:, 0:ntok], pg.t[:, 0:ntok], AF.Silu, [pg], [sg[k]])
                P.tt("dve", y1[k][:, 0:ntok], pbg.t[:, 0:ntok], ct[k][:, 0:ntok], ALU.mult, [pbg, ct[k]], [y1[k]])
                P.tt("dve", yT[:, jb, 0:ntok], y1[k][:, 0:ntok], sg[k][:, 0:ntok], ALU.mult, [y1[k], sg[k]], [yT],
                     disjoint=True)
            for jt, t in enumerate(tiles):
                P.load(xos[slots[jt]], xos[slots[jt]][:, :], x_rows(L, t))
            nxt = list(enumerate(groups[gi + 1])) if gi + 1 < len(groups) else []
            for nb in range(8):
                wo = wos[nwo % 2]
                nwo += 1
                for q in range(2):
                    P.load(wo, wo[:, q * 16:(q + 1) * 16, :],
                           WOd[j].rearrange("(kc p) n -> p kc n", p=128)[:, q * 16:(q + 1) * 16, nb * 256:(nb + 1) * 256],
                           reads=[wbuf[(("wod", j), 0)]])
                for jt, t in enumerate(tiles):
                    sl = slots[jt]
                    r = 1 if t < 2 else 0
                    pa = pbank[2 + (nacc % 4)]
                    tm = tmp[nacc % 2]
                    nacc += 1
                    for kc in range(32):
                        P.mm(pa.t[:, 0:256], yT[:, kc, jt * 128:(jt + 1) * 128], wo[:, kc, :], kc == 0, kc == 31,
                             [yT, wo], [pa])
                    cs = slice(nb * 256, (nb + 1) * 256)
                    P.tt("dve", tm[:, :], pa.t[:, 0:256], gate[r][:, cs], ALU.mult, [pa, gate[r]], [tm])
                    P.tt("dve", xos[sl][:, cs], xos[sl][:, cs], tm[:, :], ALU.add, [xos[sl], tm], [xos[sl]])
                if nb % 2 == 1 and nxt:
                    jj, tn = nxt.pop(0)
                    hT_tile(L, tn, jj, hT, xtp, xns[0], None, [0, 1])
            while nxt:
                jj, tn = nxt.pop(0)
                hT_tile(L, tn, jj, hT, xtp, xns[0], None, [0, 1])
            for jt, t in enumerate(tiles):
                sl = slots[jt]
                if is_last:
                    final_norm_store(xos[sl], t, fnw_t, junk)
                else:
                    P.store(XS[t * 128:(t + 1) * 128, :], xos[sl], xos[sl][:, :])
        P.barrier()

    for L in range(n_layers):
        if L % 2 == 0:
            even_layer(L)
        else:
            odd_layer(L)
    if n_layers < DEPTH:
        P.dma("sp", out_d, XS[CTX:, :], [xsb], [], xsb)
        P.barrier()
    return nc, P


_CACHE = {}


def _prep_inputs(inp):
    consts = _consts()
    f = lambda a: np.ascontiguousarray(np.asarray(a, dtype=np.float32))
    x = f(inp["x"])
    c = f(inp["c"])
    ctx = f(inp["ctx"])
    c_ctx = f(inp["c_ctx"])
    ada_b = f(inp["ada_b"])
    shared = {
        "ada_w": f(inp["ada_w"]),
        "ada_b": ada_b,
        "norm_wT": np.ascontiguousarray(f(inp["norm_w"]).reshape(DEPTH, 16, 128).transpose(0, 2, 1)),
        "ev_w_in": f(inp["ev_w_in"]),
        "decay": np.ascontiguousarray(f(inp["ret_decay_logit"]).reshape(2, 16)),
        "fno_w": f(inp["fno_w"]),
        "ev_w_out": f(inp["ev_w_out"]),
        "od_w_in": f(inp["od_w_in"]),
        "conv_wT": np.ascontiguousarray(f(inp["conv_w"]).reshape(2, 3, 32, 128).transpose(0, 3, 2, 1)),
        "od_w_out": f(inp["od_w_out"]),
        "fnw": np.ascontiguousarray(f(inp["final_norm_w"]).reshape(1, D)),
    }
    shared.update(consts)
    maps = []
    for b in range(x.shape[0]):
        cv = np.stack([c[b], c_ctx], axis=-1)
        m = dict(shared)
        m["x"] = x[b]
        m["ctx"] = ctx[b]
        m["cvT"] = np.ascontiguousarray(cv.reshape(16, 128, 2).transpose(1, 0, 2))
        maps.append(m)
    return maps


def kernel(**inputs):
    maps = _prep_inputs(inputs)
    if "nc" not in _CACHE:
        from contextlib import ExitStack
        nc, P = build()
        st = ExitStack()
        P.emit(st)
        _CACHE["nc"] = nc
        _CACHE["st"] = st
    nc = _CACHE["nc"]
    n = len(maps)
    res = run_bass_kernel_spmd(nc, maps, core_ids=list(range(n)))
    out = np.stack([np.asarray(r["out"]) for r in res.results], axis=0)
    return out.astype(np.float32)
```
